# Optimizing a Trainium2 kernel written in Bass

```python
import math
import jax, jax.numpy as jnp
from jax import lax
import numpy as np

D_MODEL = 1024
BATCH = 4
SEQ = 8192
DEPTH = 4

GRID_W = 64
CTX_LEN = 256
N_MIXERS = 2
N_A_LAYERS = (DEPTH + N_MIXERS - 1) // N_MIXERS
N_B_LAYERS = DEPTH // N_MIXERS
NORM_EPS = 1e-6
N_MOD = 6
CONV_W = 4
CONV_PAD = (2, 1)
LRU_WIDTH = D_MODEL
LRU_HEADS = 4
LRU_BLOCK = LRU_WIDTH // LRU_HEADS
LRU_C = 8.0
DN_QK_HEADS = 8
DN_V_HEADS = 16
DN_HEAD_K = 128
DN_HEAD_V = 128
DN_KEY_DIM = DN_QK_HEADS * DN_HEAD_K
DN_VAL_DIM = DN_V_HEADS * DN_HEAD_V
DN_CONV_DIM = 2 * DN_KEY_DIM + DN_VAL_DIM
DN_IN_DIM = DN_CONV_DIM + DN_VAL_DIM + 4 * DN_V_HEADS
DN_CHUNK = 64
D_FF = -(-8 * D_MODEL // (3 * 256)) * 256

kernel_name = 'hybrid_rglru_gdn_prefix_ctx_dit'


def rmsnorm(x, g):
    xf = x.astype(jnp.float32)
    y = xf * lax.rsqrt(jnp.mean(xf * xf, axis=-1, keepdims=True) + NORM_EPS)
    return (y * g.astype(jnp.float32)).astype(x.dtype)


def modulate(h, shift, scale):
    return h * (1 + scale) + shift


def l2norm(x):
    return x * lax.rsqrt(jnp.sum(x * x, axis=-1, keepdims=True) + NORM_EPS)


def dwconv(u, w):
    return lax.conv_general_dilated(u, w[:, None, :], window_strides=(1,), padding=[CONV_PAD],
                                    dimension_numbers=('NWC', 'WIO', 'NWC'),
                                    feature_group_count=u.shape[-1])


def to_column_major(h, n_rows):
    b, _, d = h.shape
    return h.reshape(b, n_rows, GRID_W, d).transpose(0, 2, 1, 3).reshape(b, -1, d)


def to_row_major(h, n_rows):
    b, _, d = h.shape
    return h.reshape(b, GRID_W, n_rows, d).transpose(0, 2, 1, 3).reshape(b, -1, d)


def swiglu(h, w_gu, w_down):
    gate, up = jnp.split(h @ w_gu, 2, axis=-1)
    return (jax.nn.silu(gate) * up) @ w_down


def linear_scan(a, b, h0):
    b = b.at[:, 0].add(a[:, 0] * h0)

    def combine(l, r):
        return l[0] * r[0], r[0] * l[1] + r[1]

    return lax.associative_scan(combine, (a, b), axis=1)[1]


def reverse_scan(a, b, h0):
    return jnp.flip(linear_scan(jnp.flip(a, 1), jnp.flip(b, 1), h0), 1)


def rglru_mixer(h_lat, h_ctx, w_in, b_in, conv_w, conv_b, gate_w, gate_b, lam, w_out, b_out, ctx_out):
    f32 = jnp.float32

    def branches(h):
        u, y = jnp.split(h @ w_in + b_in, 2, axis=-1)
        u = dwconv(u, conv_w) + conv_b
        return u.astype(f32), jax.nn.gelu(y)

    def coeffs(u, d):
        ub = u.reshape(u.shape[0], u.shape[1], LRU_HEADS, LRU_BLOCK)
        pre = jnp.einsum('bthi,ghij->gbthj', ub, gate_w[d].astype(f32)).reshape(2, *u.shape)
        gates = jax.nn.sigmoid(pre + gate_b[d].astype(f32)[:, None, None, :])
        r_gate, i_gate = gates[0], gates[1]
        log_a = LRU_C * r_gate * jax.nn.log_sigmoid(lam[d].astype(f32))
        a = jnp.exp(log_a)
        bx = jnp.sqrt(-jnp.expm1(2.0 * log_a)) * (i_gate * u)
        return a, bx

    u_c, y_c = branches(h_ctx)
    u_l, y_l = branches(h_lat)
    zero = jnp.zeros((h_lat.shape[0], LRU_WIDTH), f32)
    hc_f = linear_scan(*coeffs(u_c, 0), zero)
    hc_b = reverse_scan(*coeffs(u_c, 1), zero)
    hl = linear_scan(*coeffs(u_l, 0), hc_f[:, -1]) + reverse_scan(*coeffs(u_l, 1), hc_b[:, 0])
    y_lat = (hl.astype(h_lat.dtype) * y_l) @ w_out + b_out
    y_ctx = ((hc_f + hc_b).astype(h_ctx.dtype) * y_c) @ w_out + b_out if ctx_out else None
    return y_lat, y_ctx


def gated_delta_chunked(q, k, v, g, beta, s0):
    bsz, t_len, n_h, _ = q.shape
    d_v = v.shape[-1]
    n_c = t_len // DN_CHUNK

    def to_chunks(a):
        a = a.reshape(bsz, n_c, DN_CHUNK, n_h, *a.shape[3:])
        return jnp.moveaxis(a, (1, 3), (0, 2))

    incl = jnp.tril(jnp.ones((DN_CHUNK, DN_CHUNK), dtype=bool))
    strict = jnp.tril(jnp.ones((DN_CHUNK, DN_CHUNK), dtype=bool), -1)

    def step(s, xs):
        qc, kc, vc, gc, bc = xs
        gc = jnp.cumsum(gc, axis=-1)
        diff = gc[..., :, None] - gc[..., None, :]
        decay = jnp.where(incl, jnp.exp(jnp.where(incl, diff, 0.0)), 0.0)
        kb = kc * bc[..., None]
        m = jnp.where(strict, jnp.einsum('bhid,bhjd->bhij', kb, kc) * decay, 0.0)
        rhs = jnp.concatenate([vc * bc[..., None], kb * jnp.exp(gc)[..., None]], axis=-1)
        sol = lax.linalg.triangular_solve(m, rhs, left_side=True, lower=True, unit_diagonal=True)
        u, w = sol[..., :d_v], sol[..., d_v:]
        v_new = u - jnp.einsum('bhcd,bhde->bhce', w, s)
        attn = jnp.where(incl, jnp.einsum('bhid,bhjd->bhij', qc, kc) * decay, 0.0)
        o = (jnp.einsum('bhcd,bhde->bhce', qc * jnp.exp(gc)[..., None], s)
             + jnp.einsum('bhij,bhje->bhie', attn, v_new))
        g_last = gc[..., -1]
        s = (s * jnp.exp(g_last)[..., None, None]
             + jnp.einsum('bhcd,bhce->bhde', kc * jnp.exp(g_last[..., None] - gc)[..., None], v_new))
        return s, o

    s_fin, o = lax.scan(step, s0, tuple(map(to_chunks, (q, k, v, g, beta))))
    o = jnp.moveaxis(o, (0, 2), (1, 3)).reshape(bsz, t_len, n_h, d_v)
    return o, s_fin


def run_delta(q, k, v, g, beta, s0, reverse):
    if reverse:
        q, k, v, g, beta = (jnp.flip(a, 1) for a in (q, k, v, g, beta))
        o, s = gated_delta_chunked(q, k, v, g, beta, s0)
        return jnp.flip(o, 1), s
    return gated_delta_chunked(q, k, v, g, beta, s0)


def deltanet_mixer(h_lat, h_ctx, w_in, conv_w, a_log, dt_bias, norm_w, w_out, ctx_out):
    f32 = jnp.float32
    rep = DN_V_HEADS // DN_QK_HEADS

    def prep(h):
        bsz, t_len = h.shape[:2]
        qkv, z, ba = jnp.split(h @ w_in, [DN_CONV_DIM, DN_CONV_DIM + DN_VAL_DIM], axis=-1)
        qkv = jax.nn.silu(dwconv(qkv, conv_w).astype(f32))
        q, k, v = jnp.split(qkv, [DN_KEY_DIM, 2 * DN_KEY_DIM], axis=-1)
        q = jnp.repeat(l2norm(q.reshape(bsz, t_len, DN_QK_HEADS, DN_HEAD_K)), rep, axis=2) * DN_HEAD_K ** -0.5
        k = jnp.repeat(l2norm(k.reshape(bsz, t_len, DN_QK_HEADS, DN_HEAD_K)), rep, axis=2)
        v = v.reshape(bsz, t_len, DN_V_HEADS, DN_HEAD_V)
        ba = ba.astype(f32).reshape(bsz, t_len, 2, 2, DN_V_HEADS)
        beta = jax.nn.sigmoid(ba[:, :, 0])
        g = -jnp.exp(a_log.astype(f32)) * jax.nn.softplus(ba[:, :, 1] + dt_bias.astype(f32))
        return (q, k, v, g, beta), z

    def gated_out(o, z):
        bsz, t_len = o.shape[:2]
        o = o * lax.rsqrt(jnp.mean(o * o, axis=-1, keepdims=True) + NORM_EPS) * norm_w.astype(f32)
        o = o * jax.nn.silu(z.astype(f32).reshape(bsz, t_len, DN_V_HEADS, DN_HEAD_V))
        return o.reshape(bsz, t_len, DN_VAL_DIM).astype(w_out.dtype) @ w_out

    def direction(t, d):
        q, k, v, g, beta = t
        return q, k, v, g[:, :, d], beta[:, :, d]

    c_in, z_c = prep(h_ctx)
    l_in, z_l = prep(h_lat)
    s0 = jnp.zeros((h_lat.shape[0], DN_V_HEADS, DN_HEAD_K, DN_HEAD_V), f32)
    oc_f, sc_f = run_delta(*direction(c_in, 0), s0, False)
    oc_b, sc_b = run_delta(*direction(c_in, 1), s0, True)
    ol_f, _ = run_delta(*direction(l_in, 0), sc_f, False)
    ol_b, _ = run_delta(*direction(l_in, 1), sc_b, True)
    y_lat = gated_out(ol_f + ol_b, z_l).astype(h_lat.dtype)
    y_ctx = gated_out(oc_f + oc_b, z_c).astype(h_ctx.dtype) if ctx_out else None
    return y_lat, y_ctx


def setup_inputs(seed: int = 0) -> dict:
    key = jax.random.key(seed)
    ks = jax.random.split(key, 26)
    f32 = jnp.float32
    D = D_MODEL

    def nrm(k, shape, scale):
        return scale * jax.random.normal(k, shape, f32)

    lam_u = jax.random.uniform(ks[12], (N_A_LAYERS, 2, LRU_WIDTH), f32, 0.9, 0.999)
    lam_a = lam_u ** (1.0 / LRU_C)
    dt = jnp.exp(jax.random.uniform(ks[17], (N_B_LAYERS, 2, DN_V_HEADS), f32, math.log(1e-3), math.log(1e-1)))
    return {
        'x': nrm(ks[0], (BATCH, SEQ, D), 1.0),
        'c': nrm(ks[1], (BATCH, D), 1.0),
        'ctx': nrm(ks[2], (BATCH, CTX_LEN, D), 1.0),
        'c_ctx': nrm(ks[3], (D,), 1.0),
        'ada_w': nrm(ks[4], (DEPTH, D, N_MOD * D), D ** -0.5),
        'ada_b': nrm(ks[5], (DEPTH, N_MOD * D), 0.01),
        'norm_g': 1.0 + nrm(ks[6], (DEPTH, 2, D), 0.02),
        'final_norm_g': 1.0 + nrm(ks[7], (D,), 0.02),
        'rg_w_in': nrm(ks[8], (N_A_LAYERS, D, 2 * LRU_WIDTH), D ** -0.5),
        'rg_b_in': nrm(ks[9], (N_A_LAYERS, 2 * LRU_WIDTH), 0.01),
        'rg_conv_w': nrm(ks[10], (N_A_LAYERS, CONV_W, LRU_WIDTH), CONV_W ** -0.5),
        'rg_conv_b': nrm(ks[11], (N_A_LAYERS, LRU_WIDTH), 0.01),
        'rg_gate_w': nrm(ks[13], (N_A_LAYERS, 2, 2, LRU_HEADS, LRU_BLOCK, LRU_BLOCK), LRU_BLOCK ** -0.5),
        'rg_gate_b': nrm(ks[14], (N_A_LAYERS, 2, 2, LRU_WIDTH), 0.01),
        'rg_lambda': jnp.log(lam_a) - jnp.log1p(-lam_a),
        'rg_w_out': nrm(ks[15], (N_A_LAYERS, LRU_WIDTH, D), LRU_WIDTH ** -0.5),
        'rg_b_out': nrm(ks[16], (N_A_LAYERS, D), 0.01),
        'dn_w_in': nrm(ks[18], (N_B_LAYERS, D, DN_IN_DIM), D ** -0.5),
        'dn_conv_w': nrm(ks[19], (N_B_LAYERS, CONV_W, DN_CONV_DIM), CONV_W ** -0.5),
        'dn_a_log': jnp.log(jax.random.uniform(ks[20], (N_B_LAYERS, 2, DN_V_HEADS), f32, 1.0, 16.0)),
        'dn_dt_bias': dt + jnp.log(-jnp.expm1(-dt)),
        'dn_norm_w': 1.0 + nrm(ks[21], (N_B_LAYERS, DN_HEAD_V), 0.02),
        'dn_w_out': nrm(ks[22], (N_B_LAYERS, DN_VAL_DIM, D), DN_VAL_DIM ** -0.5),
        'ffn_w_gu': nrm(ks[23], (DEPTH, D, 2 * D_FF), D ** -0.5),
        'ffn_w_down': nrm(ks[24], (DEPTH, D_FF, D), D_FF ** -0.5),
    }


def reference(x, c, ctx, c_ctx, ada_w, ada_b, norm_g, final_norm_g, rg_w_in, rg_b_in, rg_conv_w,
              rg_conv_b, rg_gate_w, rg_gate_b, rg_lambda, rg_w_out, rg_b_out, dn_w_in, dn_conv_w,
              dn_a_log, dn_dt_bias, dn_norm_w, dn_w_out, ffn_w_gu, ffn_w_down):
    n_rows = x.shape[1] // GRID_W
    lat_act = jax.nn.silu(c)
    ctx_act = jax.nn.silu(c_ctx)
    for layer in range(DEPTH):
        last = layer == DEPTH - 1
        mod_l = jnp.split((lat_act @ ada_w[layer] + ada_b[layer])[:, None, :], N_MOD, axis=-1)
        mod_c = jnp.split(ctx_act @ ada_w[layer] + ada_b[layer], N_MOD, axis=-1)
        h_lat = modulate(rmsnorm(x, norm_g[layer, 0]), mod_l[0], mod_l[1])
        h_ctx = modulate(rmsnorm(ctx, norm_g[layer, 0]), mod_c[0], mod_c[1])
        j = layer // N_MIXERS
        if layer % N_MIXERS == 0:
            y_lat, y_ctx = rglru_mixer(h_lat, h_ctx, rg_w_in[j], rg_b_in[j], rg_conv_w[j], rg_conv_b[j],
                                       rg_gate_w[j], rg_gate_b[j], rg_lambda[j], rg_w_out[j], rg_b_out[j],
                                       not last)
        else:
            y_lat, y_ctx = deltanet_mixer(to_column_major(h_lat, n_rows), h_ctx, dn_w_in[j], dn_conv_w[j],
                                          dn_a_log[j], dn_dt_bias[j], dn_norm_w[j], dn_w_out[j], not last)
            y_lat = to_row_major(y_lat, n_rows)
        x = x + mod_l[2] * y_lat
        x = x + mod_l[5] * swiglu(modulate(rmsnorm(x, norm_g[layer, 1]), mod_l[3], mod_l[4]),
                                  ffn_w_gu[layer], ffn_w_down[layer])
        if not last:
            ctx = ctx + mod_c[2] * y_ctx
            ctx = ctx + mod_c[5] * swiglu(modulate(rmsnorm(ctx, norm_g[layer, 1]), mod_c[3], mod_c[4]),
                                          ffn_w_gu[layer], ffn_w_down[layer])
    return rmsnorm(x, final_norm_g)
```

```python
import contextlib
import os
import numpy as np
import concourse.bass as bass
import concourse.mybir as mybir
from concourse.bass_utils import run_bass_kernel_spmd

F32 = mybir.dt.float32
BF16 = mybir.dt.bfloat16
AF = mybir.ActivationFunctionType
ALU = mybir.AluOpType
AX = mybir.AxisListType

D = 1024
CT = 256
DFF = 2816
EPS = 1e-6


class Dep:
    __slots__ = ("w", "r", "name")

    def __init__(self, name=""):
        self.w = None
        self.r = {}
        self.name = name


class V:
    def __init__(self, ap, dep):
        self.ap = ap
        self.d = dep

    def __getitem__(self, idx):
        return V(self.ap[idx], self.d)

    def re(self, s, **kw):
        return V(self.ap.rearrange(s, **kw), self.d)


T_ = V


class Eng:
    def __init__(self, name, obj, sem):
        self.name = name
        self.obj = obj
        self.sem = sem
        self.cnt = 0
        self.seen = {}


class K:
    NDMA = 24

    def __init__(self):
        self.nc = bass.Bass("TRN2", target_bir_lowering=False)
        nc = self.nc
        self.sems = {}
        self.pe = self._eng("pe", nc.tensor)
        self.act = self._eng("act", nc.scalar)
        self.dve = self._eng("dve", nc.vector)
        self.pool = self._eng("pool", nc.gpsimd)
        self.sp = self._eng("sp", nc.sync)
        self.engs = [self.pe, self.act, self.dve, self.pool, self.sp]
        self.dsem = []
        for i in range(self.NDMA):
            s = nc.alloc_semaphore("dma%d" % i)
            self.sems["dma%d" % i] = s
            self.dsem.append(["dma%d" % i, 0])
        self.dnext = 0
        self.ssem = []
        for i in range(8):
            s = nc.alloc_semaphore("sdma%d" % i)
            self.sems["sdma%d" % i] = s
            self.ssem.append(["sdma%d" % i, 0])
        self.snext = 0
        self.ntile = 0
        self.ninst = 0
        self.rr = 0
        self.dead = False

    def _eng(self, name, obj):
        s = self.nc.alloc_semaphore("s_" + name)
        self.sems[name] = s
        return Eng(name, obj, s)

    def sb(self, shape, dt, stack=None, name=None):
        self.ntile += 1
        name = (name or "t") + "_%d" % self.ntile
        if stack is None:
            t = self.nc.alloc_sbuf_tensor(name, list(shape), dt)
        else:
            t = stack.enter_context(self.nc.sbuf_tensor(name, list(shape), dt))
        return V(t.ap(), Dep(name))

    def ps(self, shape, dt, name=None):
        self.ntile += 1
        name = (name or "p") + "_%d" % self.ntile
        t = self.nc.alloc_psum_tensor(name, list(shape), dt)
        return V(t.ap(), Dep(name))

    def dram(self, name, shape, dt, kind="Internal"):
        t = self.nc.dram_tensor(name, list(shape), dt, kind=kind)
        v = V(t.ap(), Dep(name))
        v.sub = {}
        return v

    def dv(self, t, ap, key):
        if key not in t.sub:
            t.sub[key] = Dep(str(key))
        return V(ap, t.sub[key])

    def _wait(self, eng, key, val):
        if eng.seen.get(key, 0) >= val:
            return
        eng.obj.wait_ge(self.sems[key], val)
        eng.seen[key] = val

    def _deps(self, eng, reads, writes):
        toks = {}

        def add(tok):
            if tok is None:
                return
            k, v = tok
            if k == "pe" and eng.name == "pe":
                return
            if toks.get(k, 0) < v:
                toks[k] = v
        for t in reads:
            add(t.d.w)
        for t in writes:
            add(t.d.w)
            for k, v in t.d.r.items():
                if k == eng.name:
                    continue
                add((k, v))
        for k, v in toks.items():
            self._wait(eng, k, v)

    def op(self, eng, fn, reads=(), writes=()):
        if self.dead:
            return None
        self._deps(eng, reads, writes)
        inst = fn()
        eng.cnt += 1
        self.ninst += 1
        inst.then_inc(eng.sem, 1)
        tok = (eng.name, eng.cnt)
        for t in reads:
            if t.d.r.get(eng.name, 0) < eng.cnt:
                t.d.r[eng.name] = eng.cnt
        for t in writes:
            t.d.w = tok
            t.d.r = {}
        return tok

    def dma(self, out, in_, q=None, **kw):
        eng = q or self.sp
        if self.dead:
            return None
        self._deps(eng, [in_], [out])
        if eng is self.pool:
            slot = self.ssem[self.snext]
            self.snext = (self.snext + 1) % len(self.ssem)
            if slot[1]:
                self._wait(eng, slot[0], slot[1])
        else:
            slot = self.dsem[self.dnext]
            self.dnext = (self.dnext + 1) % self.NDMA
        slot[1] += 16
        inst = eng.obj.dma_start(out=out.ap, in_=in_.ap, **kw)
        inst.then_inc(self.sems[slot[0]], 16)
        self.ninst += 1
        tok = (slot[0], slot[1])
        if in_.d.r.get(slot[0], 0) < slot[1]:
            in_.d.r[slot[0]] = slot[1]
        out.d.w = tok
        out.d.r = {}
        return tok

    def barrier(self):
        for e in self.engs:
            for name, val in self.dsem + self.ssem:
                if val:
                    self._wait(e, name, val)
            for o in self.engs:
                if o is not e and o.cnt:
                    self._wait(e, o.name, o.cnt)

    def mm(self, out, lhsT, rhs, start=True, stop=True):
        return self.op(self.pe, lambda: self.nc.tensor.matmul(out.ap, lhsT.ap, rhs.ap, start=start, stop=stop),
                       reads=[lhsT, rhs], writes=[out])

    def tr(self, out, in_, ident):
        return self.op(self.pe, lambda: self.nc.tensor.transpose(out.ap, in_.ap, ident.ap),
                       reads=[in_, ident], writes=[out])

    def actf(self, out, in_, func, bias=None, scale=None, accum=None):
        kw = {}
        reads = [in_]
        writes = [out]
        for nm, val in (("bias", bias), ("scale", scale)):
            if val is None:
                continue
            if isinstance(val, V):
                kw[nm] = val.ap
                reads.append(val)
            else:
                kw[nm] = val
        if accum is not None:
            kw["accum_out"] = accum.ap
            writes.append(accum)
        return self.op(self.act, lambda: self.nc.scalar.activation(out=out.ap, in_=in_.ap, func=func, **kw),
                       reads=reads, writes=writes)

    def tt(self, out, a, b, op, eng=None):
        e = eng or self.dve
        return self.op(e, lambda: e.obj.tensor_tensor(out=out.ap, in0=a.ap, in1=b.ap, op=op), reads=[a, b], writes=[out])

    def ts(self, out, a, s1, op0, s2=None, op1=None, eng=None):
        e = eng or self.dve
        reads = [a]
        s1a = s1.ap if isinstance(s1, V) else s1
        s2a = s2.ap if isinstance(s2, V) else s2
        if isinstance(s1, V):
            reads.append(s1)
        if isinstance(s2, V):
            reads.append(s2)
        kw = {}
        if op1 is not None:
            kw["op1"] = op1
        return self.op(e, lambda: e.obj.tensor_scalar(out=out.ap, in0=a.ap, scalar1=s1a, scalar2=s2a, op0=op0, **kw),
                       reads=reads, writes=[out])

    def stt(self, out, a, s, b, op0, op1):
        e = self.dve
        reads = [a, b]
        sa = s.ap if isinstance(s, V) else s
        if isinstance(s, V):
            reads.append(s)
        return self.op(e, lambda: e.obj.scalar_tensor_tensor(out=out.ap, in0=a.ap, scalar=sa, in1=b.ap, op0=op0, op1=op1),
                       reads=reads, writes=[out])

    def copy(self, out, in_, eng=None):
        e = eng or self.dve
        if e is self.act:
            return self.actf(out, in_, AF.Copy)
        return self.op(e, lambda: e.obj.tensor_copy(out=out.ap, in_=in_.ap), reads=[in_], writes=[out])

    def memset(self, out, val, eng=None):
        e = eng or self.dve
        return self.op(e, lambda: e.obj.memset(out.ap, val), reads=[], writes=[out])

    def scan(self, out, d0, d1, init):
        reads = [d0, d1]
        ia = init
        if isinstance(init, V):
            reads.append(init)
            ia = init.ap
        return self.op(self.dve, lambda: self.nc.vector.tensor_tensor_scan(out=out.ap, data0=d0.ap, data1=d1.ap, initial=ia,
                                                                          op0=ALU.mult, op1=ALU.add),
                       reads=reads, writes=[out])

    def recip(self, out, in_):
        return self.op(self.dve, lambda: self.nc.vector.reciprocal(out=out.ap, in_=in_.ap), reads=[in_], writes=[out])

    def reduce(self, out, in_, op=ALU.add):
        return self.op(self.dve, lambda: self.nc.vector.tensor_reduce(out=out.ap, in_=in_.ap, axis=AX.X, op=op), reads=[in_], writes=[out])

    def asel(self, out, in_, pattern, cmp, fill, base, cm):
        return self.op(self.pool, lambda: self.nc.gpsimd.affine_select(out=out.ap, in_=in_.ap, pattern=pattern, compare_op=cmp,
                                                                      fill=fill, base=base, channel_multiplier=cm),
                       reads=[in_], writes=[out])

    def alt(self):
        self.rr ^= 1
        return self.act if self.rr else self.dve


def bview(v, shape, axis):
    return V(v.ap.unsqueeze(axis).broadcast_to(list(shape)), v.d)


class StopBuild(Exception):
    pass


class Prog:
    def __init__(self, T, layers, final_norm=True, dbgL=None):
        self.T = T
        self.dbgL = dbgL
        self.layers = list(layers)
        self.final_norm = final_norm
        self.k = K()
        k = self.k
        self.lmap = {l: i for i, l in enumerate(self.layers)}
        rgl = sorted({l // 2 for l in self.layers if l % 2 == 0})
        dnl = sorted({l // 2 for l in self.layers if l % 2 == 1})
        self.rgmap = {j_: i for i, j_ in enumerate(rgl)}
        self.dnmap = {j_: i for i, j_ in enumerate(dnl)}
        NL, NR, ND = len(self.layers), max(1, len(rgl)), max(1, len(dnl))
        kin = "ExternalInput"
        self.x_in = k.dram("x", [T, D], F32, kind=kin)
        self.c_in = k.dram("ctx", [CT, D], F32, kind=kin)
        self.cvecF = k.dram("cvecF", [128, 16], F32, kind=kin)
        self.ada_w = k.dram("ada_w", [NL, D, 6 * D], F32, kind=kin)
        self.ada_b = k.dram("ada_b", [NL, 6 * D], F32, kind=kin)
        self.ada_bF = k.dram("ada_bF", [NL, 128, 48], F32, kind=kin)
        self.norm_gF = k.dram("norm_gF", [NL, 2, 128, 8], F32, kind=kin)
        self.final_g = k.dram("final_g", [1, D], F32, kind=kin)
        self.rg_w_in = k.dram("rg_w_in", [NR, D, 2 * D], F32, kind=kin)
        self.rg_b_inF = k.dram("rg_b_inF", [NR, 128, 16], F32, kind=kin)
        self.rg_conv_wF = k.dram("rg_conv_wF", [NR, 128, 32], F32, kind=kin)
        self.rg_conv_bF = k.dram("rg_conv_bF", [NR, 128, 8], F32, kind=kin)
        self.rg_gate_wL = k.dram("rg_gate_wL", [NR, 128, 32 * 256], F32, kind=kin)
        self.rg_gate_bF = k.dram("rg_gate_bF", [NR, 128, 32], F32, kind=kin)
        self.rg_lamF = k.dram("rg_lamF", [NR, 128, 16], F32, kind=kin)
        self.rg_w_out = k.dram("rg_w_out", [NR, D, D], F32, kind=kin)
        self.rg_b_out = k.dram("rg_b_out", [NR, D], F32, kind=kin)
        self.dn_w_in = k.dram("dn_w_in", [ND, D, 6208], F32, kind=kin)
        self.dn_conv_wF = k.dram("dn_conv_wF", [ND, 128, 128], F32, kind=kin)
        self.dn_alogF = k.dram("dn_alogF", [ND, 64, 1], F32, kind=kin)
        self.dn_dtbF = k.dram("dn_dtbF", [ND, 64, 1], F32, kind=kin)
        self.dn_norm_w = k.dram("dn_norm_w", [ND, 128], F32, kind=kin)
        self.dn_w_out = k.dram("dn_w_out", [ND, 2048, D], F32, kind=kin)
        self.masks_in = k.dram("masks", [7, 128, 128], F32, kind=kin)
        self.ffn_w_gu = k.dram("ffn_w_gu", [NL, D, 2 * DFF], F32, kind=kin)
        self.ffn_w_down = k.dram("ffn_w_down", [NL, DFF, D], F32, kind=kin)
        self.out = k.dram("out", [T, D], F32, kind="ExternalOutput")
        self.cout = k.dram("cout", [CT, D], F32, kind="ExternalOutput")
        TT = CT + T
        self.TT = TT
        self.xs = k.dram("xs", [T, D], F32)
        self.cs = k.dram("cs", [CT, D], F32)
        self.TP = TT + 6
        self.upre = k.dram("upre", [32, 128, self.TP], F32)
        self.yg = k.dram("yg", [8, 128, TT], BF16)
        self.hf = k.dram("hf", [8, 128, TT], F32)
        self.zs = k.dram("zs", [TT, 2048], BF16)
        self.bg = k.dram("bg", [TT, 64], F32)
        self.qT = k.dram("qT", [8, 128, TT], BF16)
        self.kT = k.dram("kT", [8, 128, TT], BF16)
        self.ktok = k.dram("ktok", [TT, 1024], BF16)
        self.vtok = k.dram("vtok", [TT, 2048], BF16)
        self.of = k.dram("of", [TT, 2048], F32)
        self.build()

    def segs(self):
        return [("c", 0, 0, CT, 256), ("l", CT, CT + 3, getattr(self, "dbgL", None) or self.T, 256)]

    def stream_rows(self, seg, src, t0, n, colmajor=False):
        k = self.k
        ten = src
        if colmajor and seg == "l":
            nrows = self.T // 64
            assert nrows % 128 == 0 and n % 128 == 0
            aps = []
            for g in range(n // 128):
                tp = t0 + g * 128
                c, r0 = tp // nrows, tp % nrows
                aps.append(ten.ap.rearrange("(r c) d -> c r d", c=64)[c, r0:r0 + 128, :])
            return aps
        aps = []
        for g in range(n // 128):
            aps.append(ten.ap[t0 + g * 128: t0 + (g + 1) * 128, :])
        return aps

    def build(self):
        k = self.k
        nc = k.nc
        with contextlib.ExitStack() as gs:
            self.gs = gs
            self.pb = [k.ps([128, 512], F32, name="bank%d" % i) for i in range(8)]
            self.ones_f = k.sb([128, 128], F32, gs, "ones")
            k.memset(self.ones_f, 1.0)
            self.ident_f = k.sb([128, 128], F32, gs, "identf")
            k.memset(self.ident_f, 0.0, eng=k.pool)
            k.asel(self.ident_f, self.ident_f, [[-1, 128]], ALU.not_equal, 1.0, 0, 1)
            self.ident_b = k.sb([128, 128], BF16, gs, "identb")
            k.copy(self.ident_b, self.ident_f)
            self.ones_b = k.sb([128, 128], BF16, gs, "onesb")
            k.copy(self.ones_b, self.ones_f)
            mk = []
            for i in range(7):
                t_ = k.sb([128, 128], F32, gs, "mk%d" % i)
                k.dma(t_, V(self.masks_in.ap[i], self.masks_in.d))
                mk.append(t_)
            self.mBD = mk[0]
            self.mU = mk[1:4]
            self.mL = mk[4:7]
            self.eps_t = k.sb([128, 1], F32, gs, "eps")
            k.memset(self.eps_t, EPS)
            self.maskI = []
            self.maskS = []
            for d in range(2):
                mi = k.sb([128, 128], F32, gs, "maskI%d" % d)
                ms = k.sb([128, 128], F32, gs, "maskS%d" % d)
                k.memset(mi, 1.0, eng=k.pool)
                k.memset(ms, 1.0, eng=k.pool)
                if d == 0:
                    k.asel(mi, mi, [[1, 128]], ALU.is_ge, 0.0, 0, -1)
                    k.asel(ms, ms, [[1, 128]], ALU.is_gt, 0.0, 0, -1)
                else:
                    k.asel(mi, mi, [[-1, 128]], ALU.is_ge, 0.0, 0, 1)
                    k.asel(ms, ms, [[-1, 128]], ALU.is_gt, 0.0, 0, 1)
                self.maskI.append(mi)
                self.maskS.append(ms)
            self.scT = k.sb([128, 8, 2], F32, gs, "scT")
            craw = k.sb([128, 16], F32, gs, "craw")
            k.dma(craw, self.cvecF)
            k.actf(self.scT.re("p a b -> p (a b)"), craw, AF.Silu)
            self.scB = k.sb([128, 8, 2, 128], F32, gs, "scB")
            k.copy(self.scB, bview(self.scT, [128, 8, 2, 128], 3))
            self.fgrow = k.sb([128, D], F32, gs, "fgrow")
            k.dma(self.fgrow, V(self.final_g.ap.partition_broadcast(128), self.final_g.d))
            zpad = k.sb([128, 32, 3], F32, gs, "zpad")
            k.memset(zpad, 0.0)
            upv = self.upre.ap.rearrange("c p t -> p c t")
            for (c0, n) in ((0, 2), (CT + 2, 3), (CT + 3 + 2 + (self.dbgL or self.T), 1)):
                if n == 1:
                    k.dma(V(upv[:, :, c0:c0 + n], self.upre.d), zpad[:, :, 0:n], allow_slow_non_contiguous=True)
                else:
                    k.dma(V(upv[:, :, c0:c0 + n], self.upre.d), zpad[:, :, 0:n])
            k.barrier()

            first = True
            for li, l in enumerate(self.layers):
                last = (li == len(self.layers) - 1)
                with contextlib.ExitStack() as ls:
                    self.ls = ls
                    self.ada_phase(l)
                    k.barrier()
                    src_x = self.x_in if first else self.xs
                    src_c = self.c_in if first else self.cs
                    if l % 2 == 0:
                        self.rg_phase(l, l // 2, src_x, src_c)
                    else:
                        self.dn_phase(l, l // 2, src_x, src_c)
                    k.barrier()
                    self.ffn_phase(l, last)
                    k.barrier()
                first = False
            k.barrier()

    def ada_phase(self, l):
        l = self.lmap[l]
        k = self.k
        ls = self.ls
        pb = self.pb
        self.modF = k.sb([128, 48, 2], F32, ls, "modF")
        self.grow = {2: k.sb([128, 2, D], F32, ls, "grow2"), 5: k.sb([128, 2, D], F32, ls, "grow5")}
        Gt = [k.sb([128, 8, 2], F32, ls, "G%d" % i) for i in range(2)]
        with contextlib.ExitStack() as st:
            wt = [k.sb([128, 8, 512], F32, st, "adaw%d" % i) for i in range(2)]
            abF = k.sb([128, 48], F32, st, "abF")
            k.dma(abF, V(self.ada_bF.ap[l], self.ada_bF.d))
            ngF = k.sb([128, 2, 8], F32, st, "ngF")
            k.dma(ngF, V(self.norm_gF.ap[l].rearrange("i p c -> p i c"), self.norm_gF.d))
            brow = k.sb([128, 2, D], F32, st, "brow")
            for wi, m in enumerate((2, 5)):
                k.dma(brow[:, wi, :], V(self.ada_b.ap[l:l + 1, m * D:(m + 1) * D].partition_broadcast(128), self.ada_b.d))
            mps = pb[0]
            mpv = V(mps.ap[:, 0:96].rearrange("p (a b) -> p a b", b=2), mps.d)
            for n in range(12):
                w = wt[n % 2]
                k.dma(w, V(self.ada_w.ap[l, :, n * 512:(n + 1) * 512].rearrange("(kc p) n -> p kc n", p=128), self.ada_w.d),
                      )
                for m in range(4):
                    oc = n * 4 + m
                    for kc in range(8):
                        k.mm(mpv[:, oc, :], w[:, kc, m * 128:(m + 1) * 128], self.scT[:, kc, :], start=(kc == 0), stop=(kc == 7))
                if n in (4, 5, 10, 11):
                    mod = 2 if n < 6 else 5
                    half = n % 2
                    wi = 0 if mod == 2 else 1
                    for v in range(2):
                        bank = pb[1 + v]
                        for kc in range(8):
                            k.mm(bank, self.scB[:, kc, v, :], w[:, kc, :], start=(kc == 0), stop=(kc == 7))
                        k.tt(self.grow[mod][:, v, half * 512:(half + 1) * 512], bank, brow[:, wi, half * 512:(half + 1) * 512], ALU.add)
            k.tt(self.modF, mpv, bview(abF, [128, 48, 2], 2), ALU.add)
            self.G = {}
            self.Sh = {}
            for i, (ms, mh) in enumerate(((1, 0), (4, 3))):
                G = Gt[i]
                k.ts(G, self.modF[:, ms * 8:(ms + 1) * 8, :], 1.0, ALU.add)
                k.tt(G, G, bview(ngF[:, i, :], [128, 8, 2], 2), ALU.mult)
                self.G[i] = G
                self.Sh[i] = self.modF[:, mh * 8:(mh + 1) * 8, :]
            k.barrier()

    def front(self, xt, G_, which, v, hT, st_tiles):
        k = self.k
        ssq, rinv, junk, xn = st_tiles
        for g in range(G_):
            k.actf(junk, xt[:, g, :], AF.Square, accum=ssq[:, g:g + 1])
        k.ts(rinv[:, 0:G_], ssq[:, 0:G_], 1.0 / D, ALU.mult, EPS, ALU.add)
        k.actf(rinv[:, 0:G_], rinv[:, 0:G_], AF.Sqrt)
        k.recip(rinv[:, 0:G_], rinv[:, 0:G_])
        for g in range(G_):
            k.actf(xn, xt[:, g, :], AF.Identity, scale=rinv[:, g:g + 1])
            bank = self.pb[6 + (g % 2)]
            pv = V(bank.ap.bitcast(BF16)[:, 0:1024].rearrange("p (a b) -> p a b", a=8), bank.d)
            for kc in range(8):
                k.tr(pv[:, kc, :], xn[:, kc * 128:(kc + 1) * 128], self.ident_b)
            for kc in range(8):
                e = k.act if g % 2 == 0 else k.dve
                o = hT[:, kc, g * 128:(g + 1) * 128]
                if e is k.act:
                    k.actf(o, pv[:, kc, :], AF.Identity, scale=self.G[which][:, kc, v:v + 1], bias=self.Sh[which][:, kc, v:v + 1])
                else:
                    k.ts(o, pv[:, kc, :], self.G[which][:, kc, v:v + 1], ALU.mult, self.Sh[which][:, kc, v:v + 1], ALU.add)

    def front_tiles(self, st):
        k = self.k
        return (k.sb([128, 4], F32, st, "ssq"), k.sb([128, 4], F32, st, "rinv"),
                k.sb([128, D], BF16, st, "junk"), k.sb([128, D], BF16, st, "xn"))

    def load_w(self, dst, src_ap, src, ncols):
        k = self.k
        c0 = 0
        while c0 < ncols:
            n = min(2048, ncols - c0)
            k.dma(dst[:, :, c0:c0 + n], V(src_ap[:, c0:c0 + n].rearrange("(kc p) n -> p kc n", p=128), src.d), q=k.pool)
            c0 += n

    def ffn_phase(self, l, last):
        l = self.lmap[l]
        k = self.k
        pb = self.pb
        with contextlib.ExitStack() as st:
            wgu = k.sb([128, 8, 2 * DFF], BF16, st, "wgu")
            wd = k.sb([128, 22, D], BF16, st, "wd")
            self.load_w(wgu, self.ffn_w_gu.ap[l], self.ffn_w_gu, 2 * DFF)
            self.load_w(wd, self.ffn_w_down.ap[l], self.ffn_w_down, D)
            ft = self.front_tiles(st)
            xts = [k.sb([128, 2, D], F32, st, "xt0")] * 2
            hT = k.sb([128, 8, 256], BF16, st, "hT")
            hid = k.sb([128, 22, 256], BF16, st, "hid")
            sg = [k.sb([128, 256], F32, st, "sg%d" % i) for i in range(2)]
            tmp = [k.sb([128, 512], F32, st, "tmp%d" % i) for i in range(2)]
            ti = 0
            for (seg, s0, p0, L, N) in self.segs():
                v = 1 if seg == "c" else 0
                src = self.cs if seg == "c" else self.xs
                G_ = N // 128
                for t0 in range(0, L, N):
                    xt = xts[ti % 2]
                    ti += 1
                    aps = self.stream_rows(seg, src, t0, N)
                    for g in range(G_):
                        k.dma(xt[:, g, :], k.dv(src, aps[g], ("ffn", seg, t0, g)))
                    self.front(xt, G_, 1, v, hT, ft)
                    for j in range(22):
                        bg_, bu_ = pb[(j % 2) * 2], pb[(j % 2) * 2 + 1]
                        for kc in range(8):
                            k.mm(bg_[:, 0:N], wgu[:, kc, j * 128:(j + 1) * 128], hT[:, kc, 0:N], start=(kc == 0), stop=(kc == 7))
                        for kc in range(8):
                            k.mm(bu_[:, 0:N], wgu[:, kc, DFF + j * 128:DFF + (j + 1) * 128], hT[:, kc, 0:N], start=(kc == 0), stop=(kc == 7))
                        s_ = sg[j % 2]
                        k.actf(s_[:, 0:N], bg_[:, 0:N], AF.Silu)
                        k.tt(hid[:, j, 0:N], s_[:, 0:N], bu_[:, 0:N], ALU.mult)
                    for g in range(G_):
                        for oc in range(2):
                            bank = pb[4 + (g * 2 + oc) % 2]
                            for j in range(22):
                                k.mm(bank, hid[:, j, g * 128:(g + 1) * 128], wd[:, j, oc * 512:(oc + 1) * 512], start=(j == 0), stop=(j == 21))
                            t_ = tmp[oc]
                            k.tt(t_, bank, self.grow[5][:, v, oc * 512:(oc + 1) * 512], ALU.mult)
                            k.tt(xt[:, g, oc * 512:(oc + 1) * 512], xt[:, g, oc * 512:(oc + 1) * 512], t_, ALU.add, eng=k.pool)
                    if last and seg == "l" and self.final_norm:
                        ssq, rinv, junk, xn = ft
                        for g in range(G_):
                            k.actf(junk, xt[:, g, :], AF.Square, accum=ssq[:, g:g + 1])
                        k.ts(rinv[:, 0:G_], ssq[:, 0:G_], 1.0 / D, ALU.mult, EPS, ALU.add)
                        k.actf(rinv[:, 0:G_], rinv[:, 0:G_], AF.Sqrt)
                        k.recip(rinv[:, 0:G_], rinv[:, 0:G_])
                        for g in range(G_):
                            k.stt(xt[:, g, :], xt[:, g, :], rinv[:, g:g + 1], self.fgrow, ALU.mult, ALU.mult)
                    dst = self.out if (last and seg == "l") else (self.cout if (last and seg == "c") else src)
                    apsd = self.stream_rows(seg, dst, t0, N)
                    for g in range(G_):
                        k.dma(k.dv(dst, apsd[g], ("ffn", seg, t0, g)), xt[:, g, :])

    def rg_phase(self, l, j, src_x, src_c):
        j = self.rgmap[j]
        k = self.k
        pb = self.pb
        T = self.T
        with contextlib.ExitStack() as st:
            b_in = k.sb([128, 16], F32, st, "b_in")
            k.dma(b_in, V(self.rg_b_inF.ap[j], self.rg_b_inF.d))
            cw = k.sb([128, 8, 4], F32, st, "cw")
            k.dma(cw.re("p a b -> p (a b)"), V(self.rg_conv_wF.ap[j], self.rg_conv_wF.d))
            cb = k.sb([128, 8], F32, st, "cb")
            k.dma(cb, V(self.rg_conv_bF.ap[j], self.rg_conv_bF.d))
            gb = k.sb([128, 4, 8], F32, st, "gb")
            k.dma(gb.re("p a b -> p (a b)"), V(self.rg_gate_bF.ap[j], self.rg_gate_bF.d))
            lam = k.sb([128, 2, 8], F32, st, "lam")
            k.dma(lam.re("p a b -> p (a b)"), V(self.rg_lamF.ap[j], self.rg_lamF.d))
            c8 = k.sb([128, 2, 8], F32, st, "c8")
            c16 = k.sb([128, 2, 8], F32, st, "c16")
            k.actf(c8, lam, AF.Exp, scale=-1.0)
            k.actf(c8, c8, AF.Ln, bias=1.0)
            k.ts(c16, c8, -16.0, ALU.mult)
            k.ts(c8, c8, -8.0, ALU.mult)
            brow = k.sb([1, D], F32, st, "brow")
            k.dma(brow, V(self.rg_b_out.ap[j:j + 1, :], self.rg_b_out.d))
            browb = k.sb([1, D], BF16, st, "browb")
            k.copy(browb, brow)
            with contextlib.ExitStack() as s1:
                w_in = k.sb([128, 8, 2 * D], BF16, s1, "w_in")
                self.load_w(w_in, self.rg_w_in.ap[j], self.rg_w_in, 2 * D)
                ft = self.front_tiles(s1)
                xts = [k.sb([128, 2, D], F32, s1, "xt%d" % i) for i in range(2)]
                hT = k.sb([128, 8, 256], BF16, s1, "hT")
                ub = [k.sb([128, 8, 256], F32, s1, "ub%d" % i) for i in range(2)]
                yb = [k.sb([128, 8, 256], BF16, s1, "yb%d" % i) for i in range(2)]
                ti = 0
                for (seg, s0, p0, L, N) in self.segs():
                    v = 1 if seg == "c" else 0
                    src = src_c if seg == "c" else src_x
                    G_ = N // 128
                    for t0 in range(0, L, N):
                        xt = xts[ti % 2]
                        u_ = ub[ti % 2]
                        y_ = yb[ti % 2]
                        ti += 1
                        aps = self.stream_rows(seg, src, t0, N)
                        for g in range(G_):
                            k.dma(xt[:, g, :], k.dv(src, aps[g], ("rg1", seg, t0, g)))
                        self.front(xt, G_, 0, v, hT, ft)
                        for oc in range(16):
                            bank = pb[oc % 4]
                            for kc in range(8):
                                k.mm(bank[:, 0:N], w_in[:, kc, oc * 128:(oc + 1) * 128], hT[:, kc, 0:N], start=(kc == 0), stop=(kc == 7))
                            if oc < 8:
                                e = k.alt()
                                if e is k.act:
                                    k.actf(u_[:, oc, 0:N], bank[:, 0:N], AF.Identity, bias=b_in[:, oc:oc + 1])
                                else:
                                    k.ts(u_[:, oc, 0:N], bank[:, 0:N], b_in[:, oc:oc + 1], ALU.add)
                            else:
                                k.actf(y_[:, oc - 8, 0:N], bank[:, 0:N], AF.Gelu_apprx_tanh, bias=b_in[:, oc:oc + 1])
                        k.dma(k.dv(self.upre, self.upre.ap[0:8, :, p0 + 2 + t0:p0 + 2 + t0 + N].rearrange("c p t -> p c t"), ("up", seg, t0)), u_[:, :, 0:N])
                        k.dma(k.dv(self.yg, self.yg.ap[:, :, s0 + t0:s0 + t0 + N].rearrange("c p t -> p c t"), ("yg", seg, t0)), y_[:, :, 0:N])
            k.barrier()
            with contextlib.ExitStack() as s2:
                gw = k.sb([128, 32, 256], BF16, s2, "gw")
                for q4 in range(4):
                    k.dma(gw[:, q4 * 8:(q4 + 1) * 8, :].re("p a b -> p (a b)"),
                          V(self.rg_gate_wL.ap[j, :, q4 * 2048:(q4 + 1) * 2048], self.rg_gate_wL.d), q=k.pool)
                w_out = k.sb([128, 8, D], BF16, s2, "w_out")
                self.load_w(w_out, self.rg_w_out.ap[j], self.rg_w_out, D)
                carry = k.sb([128, 2, 8], F32, s2, "carry")
                k.memset(carry, 0.0)
                up = [k.sb([128, 8, 259], F32, s2, "up%d" % i) for i in range(2)]
                u = k.sb([128, 8, 256], F32, s2, "u")
                ubf = k.sb([128, 8, 256], BF16, s2, "ubf")
                rg_ = k.sb([128, 8, 256], F32, s2, "rg")
                ig_ = k.sb([128, 8, 256], F32, s2, "ig")
                a_ = k.sb([128, 8, 256], F32, s2, "a")
                s_ = k.sb([128, 8, 256], F32, s2, "s")
                h_ = k.sb([128, 8, 256], F32, s2, "h")
                hfl = k.sb([128, 8, 256], F32, s2, "hfl")
                ygl = k.sb([128, 8, 256], BF16, s2, "ygl")
                m_ = k.sb([128, 8, 256], BF16, s2, "m")
                xts = [k.sb([128, 2, D], F32, s2, "xt%d" % i) for i in range(2)]
                tmp = [k.sb([128, 512], F32, s2, "tmp%d" % i) for i in range(2)]
                ti = 0
                plan = []
                for (seg, s0, p0, L, N) in self.segs():
                    tiles = list(range(0, L, N))
                    plan.append((seg, s0, p0, L, N, 0, tiles))
                    plan.append((seg, s0, p0, L, N, 1, tiles[::-1]))
                for (seg, s0, p0, L, N, d, tiles) in plan:
                    v = 1 if seg == "c" else 0
                    src = src_c if seg == "c" else src_x
                    dst = self.cs if seg == "c" else self.xs
                    G_ = N // 128
                    if seg == "l" and False:
                        pass
                    for t0 in tiles:
                        upt = up[ti % 2]
                        ti += 1
                        k.dma(upt[:, :, 0:N + 3], k.dv(self.upre, self.upre.ap[0:8, :, p0 + t0:p0 + t0 + N + 3].rearrange("c p t -> p c t"), ("up", seg, t0)))
                        if d == 1:
                            k.dma(hfl[:, :, 0:N], k.dv(self.hf, self.hf.ap[:, :, s0 + t0:s0 + t0 + N].rearrange("c p t -> p c t"), ("hf", seg, t0)))
                            k.dma(ygl[:, :, 0:N], k.dv(self.yg, self.yg.ap[:, :, s0 + t0:s0 + t0 + N].rearrange("c p t -> p c t"), ("yg", seg, t0)))
                            xt = xts[ti % 2]
                            aps = self.stream_rows(seg, src, t0, N)
                            for g in range(G_):
                                k.dma(xt[:, g, :], k.dv(src, aps[g], ("rg1", seg, t0, g)))
                        for c in range(8):
                            e = k.dve
                            k.ts(u[:, c, 0:N], upt[:, c, 0:N], cw[:, c, 0:1], ALU.mult, cb[:, c:c + 1], ALU.add)
                            for kk in range(1, 4):
                                k.stt(u[:, c, 0:N], upt[:, c, kk:kk + N], cw[:, c, kk:kk + 1], u[:, c, 0:N], ALU.mult, ALU.add)
                            k.actf(ubf[:, c, 0:N], u[:, c, 0:N], AF.Copy)
                        for g2 in range(2):
                            dstt = rg_ if g2 == 0 else ig_
                            for oc in range(8):
                                hd, oh = oc // 2, oc % 2
                                bank = pb[oc % 4]
                                for ih in range(2):
                                    widx = ((d * 2 + g2) * 4 + hd) * 2 + ih
                                    k.mm(bank[:, 0:N], gw[:, widx, oh * 128:(oh + 1) * 128], ubf[:, hd * 2 + ih, 0:N], start=(ih == 0), stop=(ih == 1))
                                k.actf(dstt[:, oc, 0:N], bank[:, 0:N], AF.Sigmoid, bias=gb[:, d * 2 + g2, oc:oc + 1])
                        for oc in range(8):
                            k.actf(a_[:, oc, 0:N], rg_[:, oc, 0:N], AF.Exp, scale=c8[:, d, oc:oc + 1])
                        for oc in range(8):
                            k.actf(s_[:, oc, 0:N], rg_[:, oc, 0:N], AF.Exp, scale=c16[:, d, oc:oc + 1])
                        for oc in range(8):
                            k.actf(s_[:, oc, 0:N], s_[:, oc, 0:N], AF.Sqrt, scale=-1.0, bias=1.0)
                        for oc in range(8):
                            k.tt(ig_[:, oc, 0:N], ig_[:, oc, 0:N], u[:, oc, 0:N], ALU.mult, eng=k.pool)
                            k.tt(s_[:, oc, 0:N], s_[:, oc, 0:N], ig_[:, oc, 0:N], ALU.mult, eng=k.pool)
                        for oc in range(8):
                            if d == 0:
                                k.scan(h_[:, oc, 0:N], a_[:, oc, 0:N], s_[:, oc, 0:N], carry[:, d, oc:oc + 1])
                            else:
                                k.scan(V(h_.ap[:, oc, 0:N][:, ::-1], h_.d), V(a_.ap[:, oc, 0:N][:, ::-1], a_.d),
                                       V(s_.ap[:, oc, 0:N][:, ::-1], s_.d), carry[:, d, oc:oc + 1])
                        if d == 0:
                            k.copy(carry[:, d, :], h_[:, :, N - 1])
                            k.dma(k.dv(self.hf, self.hf.ap[:, :, s0 + t0:s0 + t0 + N].rearrange("c p t -> p c t"), ("hf", seg, t0)), h_[:, :, 0:N])
                        else:
                            k.copy(carry[:, d, :], h_[:, :, 0])
                            k.tt(h_[:, :, 0:N], h_[:, :, 0:N], hfl[:, :, 0:N], ALU.add, eng=k.pool)
                            k.tt(m_[:, :, 0:N], h_[:, :, 0:N], ygl[:, :, 0:N], ALU.mult)
                            for g in range(G_):
                                for oc in range(2):
                                    bank = pb[4 + (g * 2 + oc) % 2]
                                    for c in range(8):
                                        k.mm(bank, m_[:, c, g * 128:(g + 1) * 128], w_out[:, c, oc * 512:(oc + 1) * 512], start=(c == 0), stop=False)
                                    k.mm(bank, self.ones_b[0:1, :], browb[0:1, oc * 512:(oc + 1) * 512], start=False, stop=True)
                                    t_ = tmp[oc]
                                    k.tt(t_, bank, self.grow[2][:, v, oc * 512:(oc + 1) * 512], ALU.mult)
                                    k.tt(xt[:, g, oc * 512:(oc + 1) * 512], xt[:, g, oc * 512:(oc + 1) * 512], t_, ALU.add, eng=k.pool)
                            apsd = self.stream_rows(seg, dst, t0, N)
                            for g in range(G_):
                                k.dma(k.dv(dst, apsd[g], ("rg3", seg, t0, g)), xt[:, g, :])

    def dn_phase(self, l, j, src_x, src_c):
        self.dn_phase_(l, j, src_x, src_c)
        self.k.dead = False
        self.k.barrier()

    def _ck(self, name):
        if os.environ.get("DBG_STOP") == name:
            self.k.dead = True

    def dn_phase_(self, l, j, src_x, src_c):
        j = self.dnmap[j]
        k = self.k
        pb = self.pb
        T = self.T
        N = 256
        with contextlib.ExitStack() as st:
            cw = k.sb([128, 32, 4], F32, st, "dcw")
            k.dma(cw.re("p a b -> p (a b)"), V(self.dn_conv_wF.ap[j], self.dn_conv_wF.d))
            alog = k.sb([64, 1], F32, st, "alog")
            dtb = k.sb([64, 1], F32, st, "dtb")
            nega = k.sb([64, 1], F32, st, "nega")
            k.dma(alog, V(self.dn_alogF.ap[j], self.dn_alogF.d))
            k.dma(dtb, V(self.dn_dtbF.ap[j], self.dn_dtbF.d))
            k.actf(nega, alog, AF.Exp)
            k.ts(nega, nega, -1.0, ALU.mult)
            nwrow = k.sb([128, 128], F32, st, "nwrow")
            k.dma(nwrow, V(self.dn_norm_w.ap[j:j + 1, :].partition_broadcast(128), self.dn_norm_w.d))
            with contextlib.ExitStack() as s1:
                w_in = k.sb([128, 8, 6208], BF16, s1, "dw_in")
                self.load_w(w_in, self.dn_w_in.ap[j], self.dn_w_in, 6208)
                ft = self.front_tiles(s1)
                xt = k.sb([128, 2, D], F32, s1, "dxt")
                hT = k.sb([128, 8, N], BF16, s1, "dhT")
                qk = [k.sb([128, 8, N], F32, s1, "dqk%d" % i) for i in range(2)]
                zt = k.sb([128, 2, 2048], BF16, s1, "dzt")
                bgT = k.sb([64, N], F32, s1, "bgT")
                e_ = k.sb([64, N], F32, s1, "e_")
                bgtok = k.sb([128, 2, 64], F32, s1, "bgtok")
                for (seg, s0, p0, L, N_) in self.segs():
                    v = 1 if seg == "c" else 0
                    src = src_c if seg == "c" else src_x
                    G_ = N // 128
                    for t0 in range(0, L, N):
                        aps = self.stream_rows(seg, src, t0, N, colmajor=True)
                        for g in range(G_):
                            k.dma(xt[:, g, :], k.dv(src, aps[g], ("dn1", seg, t0, g)))
                        self.front(xt, G_, 0, v, hT, ft)
                        for oc in range(32):
                            bank = pb[oc % 4]
                            for kc in range(8):
                                k.mm(bank[:, 0:N], w_in[:, kc, oc * 128:(oc + 1) * 128], hT[:, kc, :], start=(kc == 0), stop=(kc == 7))
                            buf = qk[(oc // 8) % 2]
                            k.copy(buf[:, oc % 8, :], bank[:, 0:N], eng=k.alt())
                            if oc % 8 == 7:
                                c0 = oc - 7
                                k.dma(k.dv(self.upre, self.upre.ap[c0:c0 + 8, :, p0 + 2 + t0:p0 + 2 + t0 + N].rearrange("c p t -> p c t"), ("dup", seg, t0, c0)), buf)
                        for g in range(G_):
                            for cc in range(4):
                                bank = pb[4 + cc % 2]
                                for kc in range(8):
                                    k.mm(bank, hT[:, kc, g * 128:(g + 1) * 128], w_in[:, kc, 4096 + cc * 512:4096 + (cc + 1) * 512], start=(kc == 0), stop=(kc == 7))
                                k.actf(zt[:, g, cc * 512:(cc + 1) * 512], bank, AF.Silu)
                            k.dma(k.dv(self.zs, self.zs.ap[s0 + t0 + g * 128:s0 + t0 + (g + 1) * 128, :], ("zs", seg, t0, g)), zt[:, g, :])
                        bank = pb[6]
                        for kc in range(8):
                            k.mm(bank[0:64, 0:N], w_in[:, kc, 6144:6208], hT[:, kc, :], start=(kc == 0), stop=(kc == 7))
                        k.actf(bgT[0:32, :], bank[0:32, 0:N], AF.Sigmoid)
                        k.actf(e_[32:64, :], bank[32:64, 0:N], AF.Exp, bias=dtb[32:64, :])
                        k.actf(e_[32:64, :], e_[32:64, :], AF.Ln, bias=1.0)
                        k.ts(bgT[32:64, :], e_[32:64, :], nega[32:64, :], ALU.mult)
                        for g in range(G_):
                            bk = pb[7]
                            k.tr(bk[:, 0:64], bgT[0:64, g * 128:(g + 1) * 128], self.ident_f[0:64, 0:64])
                            k.copy(bgtok[:, g, :], bk[:, 0:64])
                            k.dma(k.dv(self.bg, self.bg.ap[s0 + t0 + g * 128:s0 + t0 + (g + 1) * 128, :], ("bg", seg, t0, g)), bgtok[:, g, :])
            k.barrier()
            if os.environ.get("DBG_STOP") == "1a":
                return
            with contextlib.ExitStack() as s2:
                pre = [k.sb([128, 8, N + 3], F32, s2, "pre%d" % i) for i in range(2)]
                val = k.sb([128, 8, N], F32, s2, "val")
                sq = k.sb([128, N], F32, s2, "sq")
                sd = k.sb([128, N], F32, s2, "sd")
                nb = [k.sb([128, 8, N], BF16, s2, "nb%d" % i) for i in range(2)]
                ktk = k.sb([128, 2, 1024], BF16, s2, "ktk")
                vtk = k.sb([128, 2, 2048], BF16, s2, "vtk")
                ci = 0
                for (seg, s0, p0, L, N_) in self.segs():
                    G_ = N // 128
                    for t0 in range(0, L, N):
                        for cg in range(4):
                            pr = pre[ci % 2]
                            nbb = nb[ci % 2]
                            ci += 1
                            k.dma(pr, k.dv(self.upre, self.upre.ap[cg * 8:(cg + 1) * 8, :, p0 + t0:p0 + t0 + N + 3].rearrange("c p t -> p c t"), ("dup2", seg, t0, cg)))
                            for c in range(8):
                                cc = cg * 8 + c
                                k.ts(val[:, c, :], pr[:, c, 0:N], cw[:, cc, 0:1], ALU.mult)
                                for kk in range(1, 4):
                                    k.stt(val[:, c, :], pr[:, c, kk:kk + N], cw[:, cc, kk:kk + 1], val[:, c, :], ALU.mult, ALU.add)
                                k.actf(val[:, c, :], val[:, c, :], AF.Silu)
                                if cg < 2:
                                    k.tt(sq, val[:, c, :], val[:, c, :], ALU.mult, eng=k.pool)
                                    bank = pb[c % 2]
                                    k.mm(bank[:, 0:N], self.ones_f, sq)
                                    k.actf(sd, bank[:, 0:N], AF.Sqrt, bias=self.eps_t[:, 0:1])
                                    k.recip(sd, sd)
                                    k.stt(nbb[:, c, :], val[:, c, :], (128.0 ** -0.5) if cg == 0 else 1.0, sd, ALU.mult, ALU.mult)
                                else:
                                    k.copy(nbb[:, c, :], val[:, c, :], eng=k.pool)
                            if cg < 2:
                                dstT = self.qT if cg == 0 else self.kT
                                k.dma(k.dv(dstT, dstT.ap[:, :, s0 + t0:s0 + t0 + N].rearrange("c p t -> p c t"), ("qk", seg, t0)), nbb)
                            if cg >= 1:
                                for g in range(G_):
                                    bank = pb[2 + (g % 2)]
                                    pv = V(bank.ap.bitcast(BF16)[:, 0:1024].rearrange("p (a b) -> p a b", a=8), bank.d)
                                    for c in range(8):
                                        k.tr(pv[:, c, :], nbb[:, c, g * 128:(g + 1) * 128], self.ident_b)
                                    if cg == 1:
                                        k.copy(ktk[:, g, :], V(bank.ap.bitcast(BF16)[:, 0:1024], bank.d), eng=k.alt())
                                    else:
                                        k.copy(vtk[:, g, (cg - 2) * 1024:(cg - 1) * 1024], V(bank.ap.bitcast(BF16)[:, 0:1024], bank.d), eng=k.alt())
                                if cg == 1:
                                    for g in range(G_):
                                        k.dma(k.dv(self.ktok, self.ktok.ap[s0 + t0 + g * 128:s0 + t0 + (g + 1) * 128, :], ("ktok", seg, t0, g)), ktk[:, g, :])
                                if cg == 3:
                                    for g in range(G_):
                                        k.dma(k.dv(self.vtok, self.vtok.ap[s0 + t0 + g * 128:s0 + t0 + (g + 1) * 128, :], ("vtok", seg, t0, g)), vtk[:, g, :])
            k.barrier()
            if os.environ.get("DBG_STOP") == "1b":
                return
            with contextlib.ExitStack() as s3:
                w_out = k.sb([128, 16, D], BF16, s3, "dw_out")
                self.load_w(w_out, self.dn_w_out.ap[j], self.dn_w_out, D)
                S32 = k.sb([128, 16, 128], F32, s3, "S32")
                Sbf = k.sb([128, 16, 128], BF16, s3, "Sbf")

                def two(shape, dt, nm):
                    return [k.sb(shape, dt, s3, nm + "%d" % i) for i in range(2)]
                bgt = two([128, 64], F32, "bgt")
                qTt = two([128, 8, 128], BF16, "qTt")
                kTt = two([128, 8, 128], BF16, "kTt")
                ktk = two([128, 1024], BF16, "ktk2")
                vtk = two([128, 2048], BF16, "vtk2")
                gc = two([128, 16], F32, "gc")
                eg = two([128, 16], F32, "eg")
                neg = two([128, 16], F32, "neg")
                nbeta = two([128, 16], F32, "nbeta")
                Xg = two([128, 4, 128], F32, "Xg")
                dcl = two([128, 4, 128], F32, "dcl")
                dI = two([128, 4, 128], F32, "dI")
                dS = two([128, 4, 128], F32, "dS")
                egrow = two([128, 4, 128], F32, "egrow")
                Y = two([128, 4, 128], BF16, "Y")
                Z = two([128, 4, 128], BF16, "Z")
                A32 = two([128, 4, 128], F32, "A32")
                Abf = [two([128, 4, 128], BF16, "AbfA"), two([128, 4, 128], BF16, "AbfB")]
                Yc = two([128, 4, 128], BF16, "Yc")
                Zc = two([128, 4, 128], BF16, "Zc")
                Zf = k.sb([128, 4, 128], BF16, s3, "Zf")
                NM = k.sb([128, 4, 128], BF16, s3, "NM")
                ZM = k.sb([128, 4, 128], BF16, s3, "ZM")
                W1 = k.sb([128, 4, 128], BF16, s3, "W1")
                W2 = k.sb([128, 4, 128], BF16, s3, "W2")
                B32 = k.sb([128, 4, 128], F32, s3, "B32")
                Bb = two([128, 4, 128], BF16, "Bb")
                rbs = two([128, 4, 128], F32, "rbs")
                attnT = two([128, 4, 128], BF16, "attnT")
                qg = two([128, 4, 128], BF16, "qg")
                Kd = two([128, 4, 128], BF16, "Kd")
                eL = two([128, 4], F32, "eL")
                w4 = two([128, 4], F32, "w4")
                R0 = two([128, 4, 128], BF16, "R0")
                vnew = two([128, 4, 128], BF16, "vnew")
                o_sb = k.sb([128, 16, 128], F32, s3, "o_sb")
                ofl = k.sb([128, 16, 128], F32, s3, "ofl")
                sqo = k.sb([128, 16, 128], F32, s3, "sqo")
                zl = k.sb([128, 16, 128], BF16, s3, "zl")
                og = k.sb([128, 16, 128], BF16, s3, "og")
                ogT = k.sb([128, 16, 128], BF16, s3, "ogT")
                ss16 = k.sb([128, 16], F32, s3, "ss16")
                xtb = k.sb([128, D], F32, s3, "dxt2")
                tmp = [k.sb([128, 512], F32, s3, "dtmp%d" % i) for i in range(2)]

                def b3(bank):
                    return V(bank.ap.rearrange("p (a b) -> p a b", a=4), bank.d)

                def b3h(bank):
                    return V(bank.ap.bitcast(BF16)[:, 0:512].rearrange("p (a b) -> p a b", a=4), bank.d)
                bi = 0
                gi = 0
                self._ck("p2_0")
                for d in range(2):
                    k.memset(S32, 0.0)
                    k.memset(Sbf, 0.0, eng=k.pool)
                    self._ck("p2_1")
                    li = 127 if d == 0 else 0
                    blocks = []
                    for (seg, s0, p0, L, N_) in self.segs():
                        tbs = list(range(0, L, 128))
                        if d == 1:
                            tbs = tbs[::-1]
                        blocks += [(seg, s0, tb) for tb in tbs]
                    for (seg, s0, tb) in blocks:
                        v = 1 if seg == "c" else 0
                        src = src_c if seg == "c" else src_x
                        dst = self.cs if seg == "c" else self.xs
                        r0 = s0 + tb
                        B = bi % 2
                        bi += 1
                        LM = os.environ.get("DBG_LOADS", "12345")
                        if "1" in LM:
                            k.dma(bgt[B], k.dv(self.bg, self.bg.ap[r0:r0 + 128, :], ("bg2", r0)))
                        if "2" in LM:
                            k.dma(qTt[B], k.dv(self.qT, self.qT.ap[:, :, r0:r0 + 128].rearrange("c p t -> p c t"), ("qT2", r0)))
                        if "3" in LM:
                            k.dma(kTt[B], k.dv(self.kT, self.kT.ap[:, :, r0:r0 + 128].rearrange("c p t -> p c t"), ("kT2", r0)))
                        if "4" in LM:
                            k.dma(ktk[B], k.dv(self.ktok, self.ktok.ap[r0:r0 + 128, :], ("ktok2", r0)))
                        if "5" in LM:
                            k.dma(vtk[B], k.dv(self.vtok, self.vtok.ap[r0:r0 + 128, :], ("vtok2", r0)))
                        if d == 1:
                            k.dma(ofl.re("p a b -> p (a b)"), k.dv(self.of, self.of.ap[r0:r0 + 128, :], ("of", r0)))
                            k.dma(zl.re("p a b -> p (a b)"), k.dv(self.zs, self.zs.ap[r0:r0 + 128, :], ("zs2", r0)))
                        self._ck("p2_2")
                        gvec = bgt[B][:, 32 + d * 16:48 + d * 16]
                        bvec = bgt[B][:, d * 16:d * 16 + 16]
                        k.mm(pb[7][:, 0:16], self.maskI[d], gvec)
                        k.copy(gc[B], pb[7][:, 0:16])
                        k.actf(eg[B], gc[B], AF.Exp)
                        k.ts(neg[B], eg[B], -1.0, ALU.mult, eng=k.pool)
                        k.ts(nbeta[B], bvec, -1.0, ALU.mult, eng=k.pool)
                        self._ck("p2a")
                        for q in range(4):
                            Gx = gi % 2
                            gi += 1
                            h0 = 4 * q
                            k.tt(Xg[Gx], bview(self.maskI[d], [128, 4, 128], 1), bview(gvec[:, h0:h0 + 4], [128, 4, 128], 2), ALU.mult)
                            k.mm(pb[0], self.ones_f, Xg[Gx].re("p a b -> p (a b)"))
                            k.copy(rbs[Gx].re("p a b -> p (a b)"), pb[0])
                            rb3 = rbs[Gx]
                            k.actf(eL[Gx], rb3[:, :, li], AF.Exp)
                            k.tt(w4[Gx], rb3[:, :, li], gc[B][:, h0:h0 + 4], ALU.subtract)
                            k.actf(w4[Gx], w4[Gx], AF.Exp)
                            for hh in range(4):
                                k.ts(dcl[Gx][:, hh, :], rb3[:, hh, :], gc[B][:, h0 + hh:h0 + hh + 1], ALU.subtract, 0.0, ALU.min)
                            k.actf(egrow[Gx], rb3, AF.Exp)
                            k.actf(dcl[Gx], dcl[Gx], AF.Exp)
                            k.tt(dI[Gx], dcl[Gx], bview(self.maskI[d], [128, 4, 128], 1), ALU.mult)
                            k.tt(dS[Gx], dcl[Gx], bview(self.maskS[d], [128, 4, 128], 1), ALU.mult)
                            self._ck("p2b")
                            kk3 = b3(pb[1])
                            for qi in range(2):
                                qh = 2 * q + qi
                                k.mm(kk3[:, 2 * qi, :], kTt[B][:, qh, :], kTt[B][:, qh, :])
                                k.mm(kk3[:, 2 * qi + 1, :], kTt[B][:, qh, :], qTt[B][:, qh, :])
                            for hh in range(4):
                                qi = hh // 2
                                qh = 2 * q + qi
                                k.tt(qg[Gx][:, hh, :], qTt[B][:, qh, :], egrow[Gx][:, hh, :], ALU.mult, eng=k.pool)
                                k.actf(Kd[Gx][:, hh, :], ktk[B][:, qh * 128:(qh + 1) * 128], AF.Identity, scale=w4[Gx][:, hh:hh + 1])
                                k.stt(Y[0][:, hh, :], kk3[:, 2 * qi, :], nbeta[B][:, h0 + hh:h0 + hh + 1], dS[Gx][:, hh, :], ALU.mult, ALU.mult)
                                k.tt(attnT[Gx][:, hh, :], kk3[:, 2 * qi + 1, :], dI[Gx][:, hh, :], ALU.mult)
                            self._ck("p2c")
                            MN = self.mU if d == 0 else self.mL
                            MZ = self.mL if d == 0 else self.mU
                            Y0t = Y[0]
                            pz = b3h(pb[2])
                            for hh in range(4):
                                k.tr(pz[:, hh, :], Y0t[:, hh, :], self.ident_b)
                            k.copy(Zf, pz, eng=k.act)
                            k.tt(Yc[0], Y0t, bview(self.mBD, [128, 4, 128], 1), ALU.mult)
                            k.tt(Zc[0], Zf, bview(self.mBD, [128, 4, 128], 1), ALU.mult)
                            k.tt(A32[Gx], bview(self.ident_f, [128, 4, 128], 1), Yc[0], ALU.add)
                            k.copy(Abf[Gx][0], A32[Gx], eng=k.pool)
                            cur = 0
                            ac = 0
                            py, pzz, pa = b3(pb[3]), b3(pb[4]), b3(pb[5])
                            for lev in range(1, 4):
                                nxt = 1 - cur
                                if lev <= 2:
                                    for hh in range(4):
                                        k.mm(py[:, hh, :], Zc[cur][:, hh, :], Yc[cur][:, hh, :])
                                for hh in range(4):
                                    k.mm(pzz[:, hh, :], Yc[cur][:, hh, :], Zc[cur][:, hh, :])
                                k.copy(Zc[nxt], pzz, eng=k.act)
                                if lev <= 2:
                                    k.copy(Yc[nxt], py, eng=k.dve)
                                for hh in range(4):
                                    k.mm(pa[:, hh, :], Zc[nxt][:, hh, :], Abf[Gx][ac][:, hh, :])
                                k.tt(A32[Gx], A32[Gx], pa, ALU.add)
                                k.copy(Abf[Gx][1 - ac], A32[Gx], eng=k.pool)
                                cur = nxt
                                ac = 1 - ac
                            self._ck("p2d")
                            pt = b3(pb[2])
                            for hh in range(4):
                                k.tr(pt[:, hh, :], A32[Gx][:, hh, :], self.ident_f)
                            k.copy(B32, pt, eng=k.act)
                            k.copy(Bb[0], B32, eng=k.pool)
                            cb = 0
                            for m in range(3):
                                k.tt(NM, Y0t, bview(MN[m], [128, 4, 128], 1), ALU.mult)
                                k.tt(ZM, Zf, bview(MZ[m], [128, 4, 128], 1), ALU.mult)
                                p1, p2, p3 = b3(pb[3]), b3(pb[4]), b3(pb[5])
                                for hh in range(4):
                                    k.mm(p1[:, hh, :], ZM[:, hh, :], Abf[Gx][ac][:, hh, :])
                                k.copy(W1, p1, eng=k.act)
                                if m < 2:
                                    for hh in range(4):
                                        k.mm(p2[:, hh, :], NM[:, hh, :], Bb[cb][:, hh, :])
                                    k.copy(W2, p2, eng=k.dve)
                                for hh in range(4):
                                    k.mm(p3[:, hh, :], Bb[cb][:, hh, :], W1[:, hh, :])
                                k.tt(A32[Gx], A32[Gx], p3, ALU.add)
                                k.copy(Abf[Gx][1 - ac], A32[Gx], eng=k.pool)
                                if m < 2:
                                    for hh in range(4):
                                        k.mm(p1[:, hh, :], Abf[Gx][ac][:, hh, :], W2[:, hh, :])
                                    k.tt(B32, B32, p1, ALU.add)
                                    k.copy(Bb[1 - cb], B32, eng=k.pool)
                                    cb = 1 - cb
                                ac = 1 - ac
                            Af = Abf[Gx][ac]
                            self._ck("p2e")
                            pks = b3(pb[6])
                            for hh in range(4):
                                k.mm(pks[:, hh, :], kTt[B][:, 2 * q + hh // 2, :], Sbf[:, h0 + hh, :])
                            for hh in range(4):
                                h = h0 + hh
                                k.stt(R0[Gx][:, hh, :], pks[:, hh, :], neg[B][:, h:h + 1], vtk[B][:, h * 128:(h + 1) * 128], ALU.mult, ALU.add)
                            px = b3(pb[1])
                            for hh in range(4):
                                k.mm(px[:, hh, :], Af[:, hh, :], R0[Gx][:, hh, :])
                            for hh in range(4):
                                h = h0 + hh
                                k.actf(vnew[Gx][:, hh, :], px[:, hh, :], AF.Identity, scale=bvec[:, h:h + 1])
                            po = b3(pb[0])
                            for hh in range(4):
                                k.mm(po[:, hh, :], attnT[Gx][:, hh, :], vnew[Gx][:, hh, :], start=True, stop=False)
                                k.mm(po[:, hh, :], qg[Gx][:, hh, :], Sbf[:, h0 + hh, :], start=False, stop=True)
                            if d == 0:
                                k.copy(o_sb[:, h0:h0 + 4, :], po, eng=k.act)
                            else:
                                k.tt(o_sb[:, h0:h0 + 4, :], po, ofl[:, h0:h0 + 4, :], ALU.add)
                            psu = b3(pb[6])
                            for hh in range(4):
                                k.mm(psu[:, hh, :], Kd[Gx][:, hh, :], vnew[Gx][:, hh, :])
                            for hh in range(4):
                                h = h0 + hh
                                k.stt(S32[:, h, :], S32[:, h, :], eL[Gx][:, hh:hh + 1], psu[:, hh, :], ALU.mult, ALU.add)
                            k.copy(Sbf[:, h0:h0 + 4, :], S32[:, h0:h0 + 4, :], eng=k.pool)
                        self._ck("p2f")
                        if d == 0:
                            k.dma(k.dv(self.of, self.of.ap[r0:r0 + 128, :], ("of", r0)), o_sb.re("p a b -> p (a b)"))
                        else:
                            k.tt(sqo, o_sb, o_sb, ALU.mult, eng=k.pool)
                            k.reduce(ss16, sqo)
                            k.ts(ss16, ss16, 1.0 / 128.0, ALU.mult, EPS, ALU.add)
                            k.actf(ss16, ss16, AF.Sqrt)
                            k.recip(ss16, ss16)
                            k.tt(sqo, o_sb, bview(ss16, [128, 16, 128], 2), ALU.mult)
                            k.tt(sqo, sqo, bview(nwrow, [128, 16, 128], 1), ALU.mult)
                            k.tt(og, sqo, zl, ALU.mult)
                            for half in range(2):
                                bank = pb[7]
                                pv = V(bank.ap.bitcast(BF16)[:, 0:1024].rearrange("p (a b) -> p a b", a=8), bank.d)
                                for hh in range(8):
                                    k.tr(pv[:, hh, :], og[:, half * 8 + hh, :], self.ident_b)
                                k.copy(ogT[:, half * 8:(half + 1) * 8, :], pv, eng=k.act)
                            aps = self.stream_rows(seg, src, tb, 128, colmajor=True)
                            k.dma(xtb, k.dv(src, aps[0], ("dn1", seg, tb)))
                            for oc in range(2):
                                bank = pb[3 + oc]
                                for hh in range(16):
                                    k.mm(bank, ogT[:, hh, :], w_out[:, hh, oc * 512:(oc + 1) * 512], start=(hh == 0), stop=(hh == 15))
                                t_ = tmp[oc]
                                k.tt(t_, bank, self.grow[2][:, v, oc * 512:(oc + 1) * 512], ALU.mult)
                                k.tt(xtb[:, oc * 512:(oc + 1) * 512], xtb[:, oc * 512:(oc + 1) * 512], t_, ALU.add, eng=k.pool)
                            apd = self.stream_rows(seg, dst, tb, 128, colmajor=True)
                            k.dma(k.dv(dst, apd[0], ("dn3", seg, tb)), xtb)


def _vecF(v):
    n = v.shape[-1] // 128
    return np.ascontiguousarray(np.swapaxes(v.reshape(v.shape[:-1] + (n, 128)), -1, -2))


def prep_inputs(inp, b, T, layers=None):
    f = np.float32
    m = {}
    m["x"] = np.ascontiguousarray(inp["x"][b, :T])
    m["ctx"] = np.ascontiguousarray(inp["ctx"][b])
    cv = np.stack([inp["c"][b], inp["c_ctx"]], 0)
    m["cvecF"] = np.ascontiguousarray(cv.reshape(2, 8, 128).transpose(2, 1, 0).reshape(128, 16))
    m["ada_w"] = inp["ada_w"]
    m["ada_b"] = inp["ada_b"]
    m["ada_bF"] = np.ascontiguousarray(inp["ada_b"].reshape(4, 48, 128).transpose(0, 2, 1))
    m["norm_gF"] = np.ascontiguousarray(inp["norm_g"].reshape(4, 2, 8, 128).transpose(0, 1, 3, 2))
    m["final_g"] = inp["final_norm_g"].reshape(1, D)
    m["rg_w_in"] = inp["rg_w_in"]
    m["rg_b_inF"] = np.ascontiguousarray(inp["rg_b_in"].reshape(2, 16, 128).transpose(0, 2, 1))
    m["rg_conv_wF"] = np.ascontiguousarray(inp["rg_conv_w"].reshape(2, 4, 8, 128).transpose(0, 3, 2, 1).reshape(2, 128, 32))
    m["rg_conv_bF"] = np.ascontiguousarray(inp["rg_conv_b"].reshape(2, 8, 128).transpose(0, 2, 1))
    gw = inp["rg_gate_w"].reshape(2, 2, 2, 4, 2, 128, 256)
    m["rg_gate_wL"] = np.ascontiguousarray(gw.transpose(0, 5, 1, 2, 3, 4, 6).reshape(2, 128, 32 * 256))
    m["rg_gate_bF"] = np.ascontiguousarray(inp["rg_gate_b"].reshape(2, 4, 8, 128).transpose(0, 3, 1, 2).reshape(2, 128, 32))
    m["rg_lamF"] = np.ascontiguousarray(inp["rg_lambda"].reshape(2, 2, 8, 128).transpose(0, 3, 1, 2).reshape(2, 128, 16))
    m["rg_w_out"] = inp["rg_w_out"]
    m["rg_b_out"] = inp["rg_b_out"]
    m["dn_w_in"] = inp["dn_w_in"]
    m["dn_conv_wF"] = np.ascontiguousarray(inp["dn_conv_w"].reshape(2, 4, 32, 128).transpose(0, 3, 2, 1).reshape(2, 128, 128))
    al = np.zeros((2, 64, 1), f)
    al[:, 32:, 0] = inp["dn_a_log"].reshape(2, 32)
    dt = np.zeros((2, 64, 1), f)
    dt[:, 32:, 0] = inp["dn_dt_bias"].reshape(2, 32)
    m["dn_alogF"] = al
    m["dn_dtbF"] = dt
    m["dn_norm_w"] = inp["dn_norm_w"]
    m["dn_w_out"] = inp["dn_w_out"]
    idx = np.arange(128)
    mks = [(idx[:, None] // 16 == idx[None, :] // 16)]
    for b_ in (16, 32, 64):
        same = idx[:, None] // (2 * b_) == idx[None, :] // (2 * b_)
        lo = (idx % (2 * b_)) < b_
        mks.append(same & lo[:, None] & ~lo[None, :])
    mks += [mks[1].T, mks[2].T, mks[3].T]
    m["masks"] = np.stack(mks, 0).astype(f)
    m["ffn_w_gu"] = inp["ffn_w_gu"]
    m["ffn_w_down"] = inp["ffn_w_down"]
    if layers is not None:
        ls = list(layers)
        rgl = sorted({l // 2 for l in ls if l % 2 == 0}) or [0]
        dnl = sorted({l // 2 for l in ls if l % 2 == 1}) or [0]
        for k_ in list(m.keys()):
            if k_ in ("ada_w", "ada_b", "ada_bF", "norm_gF", "ffn_w_gu", "ffn_w_down"):
                m[k_] = m[k_][ls]
            elif k_.startswith("rg_"):
                m[k_] = m[k_][rgl]
            elif k_.startswith("dn_"):
                m[k_] = m[k_][dnl]
    return {k_: np.ascontiguousarray(v_, dtype=f) for k_, v_ in m.items()}


def kernel(**inputs):
    B, T, _ = inputs["x"].shape
    prog = Prog(T, [0, 1, 2, 3])
    in_maps = [prep_inputs(inputs, b, T) for b in range(B)]
    res = run_bass_kernel_spmd(prog.k.nc, in_maps, core_ids=list(range(B)))
    out = np.stack([res.results[b]["out"] for b in range(B)], 0)
    return out.astype(np.float32)
```
